# Optimizing a Trainium2 kernel written in Bass

```python
import jax, jax.numpy as jnp
from jax import lax
import numpy as np

D_MODEL = 2048
BATCH = 4
SEQ = 2048
DEPTH = 4
DEC_BATCH = 128
DEC_SEQ = 1
PAST_LEN = 16384
PAGE_SIZE = 128

N_MIXERS = 3
N_HGRN_LAYERS = (DEPTH + 2) // 3
N_GLA_LAYERS = (DEPTH + 1) // 3
N_POOL_LAYERS = DEPTH // 3

HG_HEADS = 16
HG_DK = 128
HG_DV = D_MODEL // HG_HEADS
HG_F = HG_HEADS * HG_DK
HG_I = HG_HEADS * HG_DV

GLA_HEADS = 4
GLA_K = D_MODEL // 2
GLA_V = D_MODEL
GLA_DK = GLA_K // GLA_HEADS
GLA_DV = GLA_V // GLA_HEADS
GLA_GATE_RANK = 16
GLA_GATE_NORMALIZER = 16.0

POOL_WINDOWS = (2, 4, 8, 16)
POOL_GROUPS = len(POOL_WINDOWS)
POOL_GC = D_MODEL // POOL_GROUPS
POOL_BUF = max(POOL_WINDOWS) - 1

N_MEM = 256
XA_HEADS = 4
XA_DH = D_MODEL // XA_HEADS

D_FF = 4 * D_MODEL
CHUNK = 32
N_NORMS = 6
EPS = 1e-6

kernel_name = 'hybrid_hgrn2_gla_pool_memxattn_decode_step'


def rmsnorm(x, g):
    xf = x.astype(jnp.float32)
    y = xf * lax.rsqrt(jnp.mean(xf * xf, axis=-1, keepdims=True) + EPS)
    return (y * g.astype(jnp.float32)).astype(x.dtype)


def gated_head_rmsnorm(o, gate, g):
    o = o.astype(jnp.float32)
    y = o * lax.rsqrt(jnp.mean(o * o, axis=-1, keepdims=True) + EPS) * g.astype(jnp.float32)
    return y * jax.nn.silu(gate.astype(jnp.float32))


def chunk_gated_linear(q, k, v, logg, s0):
    B, L, H, _ = q.shape
    dv = v.shape[-1]
    c = min(CHUNK, L)
    n = -(-L // c)
    pad = n * c - L

    def to_chunks(t):
        t = jnp.pad(t, ((0, 0), (0, pad), (0, 0), (0, 0)))
        return t.reshape(B, n, c, H, t.shape[-1]).transpose(1, 0, 3, 2, 4)

    xs = tuple(to_chunks(t.astype(jnp.float32)) for t in (q, k, v, logg))
    causal = jnp.asarray(np.tril(np.ones((c, c), dtype=bool)))[:, :, None]

    def step(state, inp):
        qc, kc, vc, gc = inp
        b = jnp.cumsum(gc, axis=2)
        b_last = b[:, :, -1:, :]
        o_inter = jnp.einsum('bhtk,bhkv->bhtv', qc * jnp.exp(b), state)
        rel = b[:, :, :, None, :] - b[:, :, None, :, :]
        decay = jnp.where(causal, jnp.exp(jnp.where(causal, rel, 0.0)), 0.0)
        scores = jnp.einsum('bhtk,bhsk,bhtsk->bhts', qc, kc, decay)
        o = o_inter + jnp.einsum('bhts,bhsv->bhtv', scores, vc)
        state = (jnp.exp(b_last)[:, :, 0, :, None] * state
                 + jnp.einsum('bhsk,bhsv->bhkv', kc * jnp.exp(b_last - b), vc))
        return state, o

    state, o = lax.scan(step, s0.astype(jnp.float32), xs)
    o = o.transpose(1, 0, 3, 2, 4).reshape(B, n * c, H, dv)[:, :L]
    return o, state


def hgrn_lower_bounds(lb_logits):
    p = jax.nn.softmax(lb_logits.astype(jnp.float32), axis=0)
    return jnp.cumsum(p, axis=0) - p[0]


def hgrn2_mixer(u, s0, lb, w_in, g_norm, w_o):
    B, L, _ = u.shape
    proj = u @ w_in
    q, fz, i, g = jnp.split(proj, [HG_F, 2 * HG_F, 2 * HG_F + HG_I], axis=-1)
    q = jax.nn.silu(q.astype(jnp.float32)).reshape(B, L, HG_HEADS, HG_DK)
    fz = fz.astype(jnp.float32).reshape(B, L, HG_HEADS, HG_DK)
    lb = lb.astype(jnp.float32).reshape(HG_HEADS, HG_DK)
    ls = jax.nn.log_sigmoid(fz)
    pos = lb > 0
    lb_safe = jnp.where(pos, lb, 1.0)
    logf = jnp.where(pos, jnp.logaddexp(jnp.log(lb_safe), jnp.log1p(-lb) + ls), ls)
    k = (1.0 - lb) * jax.nn.sigmoid(-fz)
    v = i.reshape(B, L, HG_HEADS, HG_DV)
    o, s = chunk_gated_linear(q, k, v, logf, s0)
    o = gated_head_rmsnorm(o, g.reshape(B, L, HG_HEADS, HG_DV), g_norm)
    return o.reshape(B, L, HG_I).astype(u.dtype) @ w_o, s


def gla_mixer(u, s0, w_in, w_gk1, w_gk2, b_gk, g_norm, w_o):
    B, L, _ = u.shape
    proj = u @ w_in
    q, k, v, g = jnp.split(proj, [GLA_K, 2 * GLA_K, 2 * GLA_K + GLA_V], axis=-1)
    gk = (u @ w_gk1) @ w_gk2 + b_gk
    logg = (jax.nn.log_sigmoid(gk.astype(jnp.float32)) / GLA_GATE_NORMALIZER).reshape(B, L, GLA_HEADS, GLA_DK)
    q = q.astype(jnp.float32).reshape(B, L, GLA_HEADS, GLA_DK) * (GLA_DK ** -0.5)
    k = k.reshape(B, L, GLA_HEADS, GLA_DK)
    v = v.reshape(B, L, GLA_HEADS, GLA_DV)
    o, s = chunk_gated_linear(q, k, v, logg, s0)
    o = gated_head_rmsnorm(o, g.reshape(B, L, GLA_HEADS, GLA_DV), g_norm)
    return o.reshape(B, L, GLA_V).astype(u.dtype) @ w_o, s


def pool_mixer(u, buf, w, scale):
    B, L, D = u.shape
    P = buf.shape[1]
    full = jnp.concatenate([buf.astype(u.dtype), u], axis=1)
    T = P + L
    cs = jnp.concatenate([jnp.zeros((B, 1, D), jnp.float32),
                          jnp.cumsum(full.astype(jnp.float32), axis=1)], axis=1)
    cs = cs.reshape(B, T + 1, POOL_GROUPS, POOL_GC)
    win = np.array(POOL_WINDOWS)
    end = np.arange(L) + P + 1
    start = np.maximum(end[:, None] - win[None, :], 0)
    cnt = jnp.asarray((end[:, None] - start).astype(np.float32))
    grp = np.arange(POOL_GROUPS)
    wsum = cs[:, end] - cs[:, start, grp]
    d = wsum / cnt[None, :, :, None] - u.astype(jnp.float32).reshape(B, L, POOL_GROUPS, POOL_GC)
    y = jnp.einsum('blgc,gcd->blgd', d.astype(u.dtype), w).reshape(B, L, D) * scale
    new_buf = full[:, max(T - POOL_BUF, 0):]
    return y, new_buf


def memory_kv(mem, gain, w_k, w_v):
    B = mem.shape[0]
    mn = rmsnorm(mem, gain)
    k = (mn @ w_k).reshape(B, N_MEM, XA_HEADS, XA_DH)
    v = (mn @ w_v).reshape(B, N_MEM, XA_HEADS, XA_DH)
    return k, v


def cross_attention(u, k, v, w_q, w_o):
    B, L, _ = u.shape
    q = (u @ w_q).reshape(B, L, XA_HEADS, XA_DH)
    s = jnp.einsum('blhd,bmhd->bhlm', q.astype(jnp.float32), k.astype(jnp.float32)) * (XA_DH ** -0.5)
    p = jax.nn.softmax(s, axis=-1)
    o = jnp.einsum('bhlm,bmhd->blhd', p, v.astype(jnp.float32))
    return o.reshape(B, L, D_MODEL).astype(u.dtype) @ w_o


def squared_relu_mlp(u, w_up, w_down):
    return jnp.square(jax.nn.relu(u @ w_up)) @ w_down


def trunk(x, s_hgrn, s_gla, buf_pool, mem_k, mem_v, norm_gains,
          hgrn_w_in, hgrn_lb, hgrn_g_norm, hgrn_w_o,
          gla_w_in, gla_w_gk1, gla_w_gk2, gla_b_gk, gla_g_norm, gla_w_o,
          pool_w, pool_scale, xa_w_q, xa_w_o, mlp_w_up, mlp_w_down):
    lbs = hgrn_lower_bounds(hgrn_lb)
    new_h, new_g, new_p = [], [], []
    for i in range(DEPTH):
        j = i // N_MIXERS
        kind = i % N_MIXERS
        u = rmsnorm(x, norm_gains[i, 0])
        if kind == 0:
            m, s = hgrn2_mixer(u, s_hgrn[j], lbs[i], hgrn_w_in[j], hgrn_g_norm[j], hgrn_w_o[j])
            new_h.append(s.astype(s_hgrn.dtype))
        elif kind == 1:
            m, s = gla_mixer(u, s_gla[j], gla_w_in[j], gla_w_gk1[j], gla_w_gk2[j], gla_b_gk[j],
                             gla_g_norm[j], gla_w_o[j])
            new_g.append(s.astype(s_gla.dtype))
        else:
            m, b = pool_mixer(u, buf_pool[j], pool_w[j], pool_scale[j])
            new_p.append(b.astype(buf_pool.dtype))
        x = x + rmsnorm(m, norm_gains[i, 1])
        u = rmsnorm(x, norm_gains[i, 2])
        x = x + rmsnorm(cross_attention(u, mem_k[i], mem_v[i], xa_w_q[i], xa_w_o[i]), norm_gains[i, 3])
        u = rmsnorm(x, norm_gains[i, 4])
        x = x + rmsnorm(squared_relu_mlp(u, mlp_w_up[i], mlp_w_down[i]), norm_gains[i, 5])
    return x, jnp.stack(new_h), jnp.stack(new_g), jnp.stack(new_p)


def setup_inputs(seed: int = 0) -> dict:
    key = jax.random.key(seed)
    ks = jax.random.split(key, 28)

    def nrm(i, shape, scale):
        return scale * jax.random.normal(ks[i], shape, jnp.float32)

    D = D_MODEL
    pool_rows = min(POOL_BUF, PAST_LEN)
    return {
        'x_prompt': nrm(0, (BATCH, SEQ, D), 1.0),
        'x_sample': nrm(1, (DEC_BATCH, DEC_SEQ, D), 1.0),
        'state_hgrn': nrm(2, (N_HGRN_LAYERS, DEC_BATCH, HG_HEADS, HG_DK, HG_DV), 0.5),
        'state_gla': nrm(3, (N_GLA_LAYERS, DEC_BATCH, GLA_HEADS, GLA_DK, GLA_DV), 0.5),
        'state_pool': nrm(4, (N_POOL_LAYERS, DEC_BATCH, pool_rows, D), 1.0),
        'cache_mem_k': nrm(5, (DEPTH, DEC_BATCH, N_MEM, XA_HEADS, XA_DH), 1.0),
        'cache_mem_v': nrm(6, (DEPTH, DEC_BATCH, N_MEM, XA_HEADS, XA_DH), 1.0),
        'mem_prompt': nrm(7, (BATCH, N_MEM, D), 1.0),
        'norm_gains': 1.0 + nrm(8, (DEPTH, N_NORMS, D), 0.1),
        'hgrn_w_in': nrm(9, (N_HGRN_LAYERS, D, 2 * HG_F + 2 * HG_I), D ** -0.5),
        'hgrn_lb': nrm(10, (DEPTH, HG_F), 1.0),
        'hgrn_g_norm': 1.0 + nrm(11, (N_HGRN_LAYERS, HG_DV), 0.1),
        'hgrn_w_o': nrm(12, (N_HGRN_LAYERS, HG_I, D), HG_I ** -0.5),
        'gla_w_in': nrm(13, (N_GLA_LAYERS, D, 2 * GLA_K + 2 * GLA_V), D ** -0.5),
        'gla_w_gk1': nrm(14, (N_GLA_LAYERS, D, GLA_GATE_RANK), D ** -0.5),
        'gla_w_gk2': nrm(15, (N_GLA_LAYERS, GLA_GATE_RANK, GLA_K), GLA_GATE_RANK ** -0.5),
        'gla_b_gk': nrm(16, (N_GLA_LAYERS, GLA_K), 0.1),
        'gla_g_norm': 1.0 + nrm(17, (N_GLA_LAYERS, GLA_DV), 0.1),
        'gla_w_o': nrm(18, (N_GLA_LAYERS, GLA_V, D), GLA_V ** -0.5),
        'pool_w': nrm(19, (N_POOL_LAYERS, POOL_GROUPS, POOL_GC, POOL_GC), POOL_GC ** -0.5),
        'pool_scale': 1.0 + nrm(20, (N_POOL_LAYERS, D), 0.1),
        'mem_norm': 1.0 + nrm(21, (DEPTH, D), 0.1),
        'xa_w_q': nrm(22, (DEPTH, D, D), D ** -0.5),
        'xa_w_k': nrm(23, (DEPTH, D, D), D ** -0.5),
        'xa_w_v': nrm(24, (DEPTH, D, D), D ** -0.5),
        'xa_w_o': nrm(25, (DEPTH, D, D), D ** -0.5),
        'mlp_w_up': nrm(26, (DEPTH, D, D_FF), D ** -0.5),
        'mlp_w_down': nrm(27, (DEPTH, D_FF, D), D_FF ** -0.5),
    }


def reference(x_prompt, x_sample, state_hgrn, state_gla, state_pool, cache_mem_k, cache_mem_v,
              mem_prompt, norm_gains, hgrn_w_in, hgrn_lb, hgrn_g_norm, hgrn_w_o,
              gla_w_in, gla_w_gk1, gla_w_gk2, gla_b_gk, gla_g_norm, gla_w_o,
              pool_w, pool_scale, mem_norm, xa_w_q, xa_w_k, xa_w_v, xa_w_o,
              mlp_w_up, mlp_w_down):
    weights = (norm_gains, hgrn_w_in, hgrn_lb, hgrn_g_norm, hgrn_w_o,
               gla_w_in, gla_w_gk1, gla_w_gk2, gla_b_gk, gla_g_norm, gla_w_o,
               pool_w, pool_scale, xa_w_q, xa_w_o, mlp_w_up, mlp_w_down)

    mk, mv = [], []
    for i in range(DEPTH):
        k, v = memory_kv(mem_prompt, mem_norm[i], xa_w_k[i], xa_w_v[i])
        mk.append(k)
        mv.append(v)
    mem_k_prompt = jnp.stack(mk)
    mem_v_prompt = jnp.stack(mv)

    Bp = x_prompt.shape[0]
    h0 = jnp.zeros((N_HGRN_LAYERS, Bp, HG_HEADS, HG_DK, HG_DV), state_hgrn.dtype)
    g0 = jnp.zeros((N_GLA_LAYERS, Bp, GLA_HEADS, GLA_DK, GLA_DV), state_gla.dtype)
    p0 = jnp.zeros((N_POOL_LAYERS, Bp, 0, D_MODEL), state_pool.dtype)
    y_prompt, h_p, g_p, p_p = trunk(x_prompt, h0, g0, p0, mem_k_prompt, mem_v_prompt, *weights)

    y_sample, h_s, g_s, p_s = trunk(x_sample, state_hgrn, state_gla, state_pool,
                                    cache_mem_k, cache_mem_v, *weights)
    return (y_prompt, y_sample, h_p, h_s, g_p, g_s, p_p, p_s, mem_k_prompt, mem_v_prompt)
```

```python
import contextlib
import numpy as np
import concourse.bass as bass
import concourse.mybir as mybir
from concourse.bass_utils import run_bass_kernel_spmd

F32 = mybir.dt.float32
BF16 = mybir.dt.bfloat16
AF = mybir.ActivationFunctionType
ALU = mybir.AluOpType
AX = mybir.AxisListType

D = 2048
KC = 16
T = 512
NS = 16
PF = 15
SEQ = 2048
NMEM = 256
EPS = 1e-6
DFF = 8192


class R:
    __slots__ = ("w", "rd", "dsem", "dcnt", "poolq")

    def __init__(self):
        self.w = None
        self.rd = {}
        self.dsem = None
        self.dcnt = 0
        self.poolq = False


class Buf:
    def __init__(self, t, n=1):
        self.t = t
        self.rs = [R() for _ in range(n)]
        self.R = self.rs[0]


class Rot:
    def __init__(self, bufs):
        self.bufs = bufs
        self.i = 0

    def next(self):
        b = self.bufs[self.i % len(self.bufs)]
        self.i += 1
        return b


class KB:
    def __init__(self, nc, es):
        self.nc = nc
        self.es = es
        self.eng = {"pe": nc.tensor, "act": nc.scalar, "dve": nc.vector, "pool": nc.gpsimd, "sp": nc.sync}
        self.sem = {n: es.enter_context(nc.semaphore("s_" + n)) for n in self.eng}
        self.cnt = {n: 0 for n in self.eng}
        self.seen = {n: {} for n in self.eng}
        self.owners = []
        self.uid = 0
        self.pes = None

    def name(self, p):
        self.uid += 1
        return "%s%d" % (p, self.uid)

    def sb(self, shape, dt, n=1, name="b"):
        return Buf(self.nc.alloc_sbuf_tensor(self.name(name), list(shape), dt), n)

    def rot(self, shape, dt, cnt, name="r"):
        return Rot([self.sb(shape, dt, 1, name) for _ in range(cnt)])

    def ph(self, shape, dt, n=1, name="p"):
        return Buf(self.pes.enter_context(self.nc.sbuf_tensor(self.name(name), list(shape), dt)), n)

    def phrot(self, shape, dt, cnt, name="pr"):
        return Rot([self.ph(shape, dt, 1, name) for _ in range(cnt)])

    def _wait(self, e, sem, val):
        k = id(sem)
        if self.seen[e].get(k, 0) < val:
            self.eng[e].wait_ge(sem, val)
            self.seen[e][k] = val

    def _deps(self, e, reads, writes):
        d = {}

        def add(dep):
            if dep is None:
                return
            k = id(dep[0])
            if k not in d or d[k][1] < dep[1]:
                d[k] = dep

        for r in reads:
            add(r.w)
        for w in writes:
            add(w.w)
            for dep in w.rd.values():
                add(dep)
        own = id(self.sem[e])
        for k, (sem, val) in d.items():
            if e == "pe" and k == own:
                continue
            self._wait(e, sem, val)

    def op(self, e, fn, reads=(), writes=()):
        self._deps(e, reads, writes)
        inst = fn(self.eng[e])
        self.cnt[e] += 1
        inst.then_inc(self.sem[e], 1)
        dep = (self.sem[e], self.cnt[e])
        k = id(dep[0])
        for r in reads:
            r.rd[k] = dep
        for w in writes:
            w.w = dep
            w.rd = {}

    def dma(self, q, pairs, reads=(), writes=(), owner=None):
        owner = owner or (writes[0] if writes else reads[0])
        if owner.dsem is None:
            owner.dsem = self.es.enter_context(self.nc.semaphore(self.name("d")))
            self.owners.append(owner)
            owner.poolq = (q == "pool")
        self._deps(q, reads, writes)
        for o, i in pairs:
            self.eng[q].dma_start(out=o, in_=i).then_inc(owner.dsem, 16)
            owner.dcnt += 16
        dep = (owner.dsem, owner.dcnt)
        k = id(owner.dsem)
        for r in reads:
            r.rd[k] = dep
        for w in writes:
            w.w = dep
            w.rd = {}

    def barrier(self, engines=("pe", "act", "dve", "sp")):
        for e in engines:
            for o in ("pe", "act", "dve", "sp"):
                if o != e and self.cnt[o] > 0:
                    self._wait(e, self.sem[o], self.cnt[o])
            for ow in self.owners:
                if ow.dcnt > 0 and not ow.poolq:
                    self._wait(e, ow.dsem, ow.dcnt)

    @contextlib.contextmanager
    def phase(self):
        self.pes = contextlib.ExitStack()
        try:
            yield
        finally:
            self.barrier()
            self.pes.close()
            self.pes = None


def build(cfg):
    NT = cfg.get("nt", 4)
    NL = cfg.get("nl", 4)
    SAMPLE = cfg.get("sample", True)
    LT = cfg.get("ltypes", ["hgrn", "gla", "pool", "hgrn"])
    NLW = cfg.get("nlw", 4)
    NH = min(2, NLW)
    nc = bass.Bass("TRN2", target_bir_lowering=False)
    es = contextlib.ExitStack()
    kb = KB(nc, es)
    op, dma = kb.op, kb.dma

    def din(name, shape, dt=F32):
        return nc.dram_tensor(name, list(shape), dt, kind="ExternalInput").ap()

    def dout(name, shape, dt=F32):
        return nc.dram_tensor(name, list(shape), dt, kind="ExternalOutput").ap()

    xin = din("xT", [128, KC, SEQ])
    xsin = din("xsT", [128, KC, NS])
    memin = din("memT", [128, KC, NMEM])
    gains_d = din("gains", [128, 24, KC])
    memg_d = din("memg", [128, 4, KC])
    lb_d = din("lbT", [128, 4, 16])
    hgn_d = din("hgn", [128, 2])
    ggn_d = din("ggn", [128, 4])
    gbk_d = din("gbk", [128, 8])
    psc_d = din("psc", [128, KC])
    cst_d = din("cst", [128, 6, 128])
    cm32_d = din("cm32", [32, 16, 32])
    rst_d = din("rst", [128, T])
    icnt_d = din("icnt", [128, 4, 16])
    dmask_d = din("dmask", [16, 16, 128])
    psel_d = din("psel", [120, 2, 4, NS])
    hg_w_in = din("hgrn_w_in", [NH, D, 8192])
    hg_w_o = din("hgrn_w_o", [NH, D, D])
    gl_w_in = din("gla_w_in", [1, D, 6144])
    gl_w1 = din("gla_w_gk1", [1, D, 16])
    gl_w2 = din("gla_w_gk2", [1, 16, 1024])
    gl_w_o = din("gla_w_o", [1, D, D])
    pl_w = din("pool_w", [1, 4, 512, 512])
    xa_wq = din("xa_w_q", [NLW, D, D])
    xa_wk = din("xa_w_k", [NLW, D, D])
    xa_wv = din("xa_w_v", [NLW, D, D])
    xa_wo = din("xa_w_o", [NLW, D, D])
    w_up = din("mlp_w_up", [NLW, D, DFF])
    w_dn = din("mlp_w_down", [NLW, DFF, D])
    st_h = din("state_hgrn", [2, NS, 16, 128, 128])
    st_g = din("state_gla", [1, NS, 4, 256, 512])
    st_p = din("state_pool", [1, NS, 15, D])
    ck = din("cache_k", [NLW, NS, NMEM, D])
    cv = din("cache_v", [NLW, NS, NMEM, D])

    y_out = dout("yT", [128, KC, SEQ])
    ys_out = dout("ysT", [128, KC, NS])
    hp_out = dout("hp", [2, 16, 128, 128])
    hs_out = dout("hs", [2, NS, 16, 128, 128])
    gp_out = dout("gp", [1, 4, 256, 512])
    gs_out = dout("gs", [1, NS, 4, 256, 512])
    pp_out = dout("ppT", [128, KC, 15])
    ps_shift = dout("ps_shift", [NS, 14, D])
    ps_new = dout("ps_newT", [128, KC, NS])
    mk_out = dout("mk", [4, NMEM, D])
    mv_out = dout("mv", [4, NMEM, D])
    kT_scr = dout("kT_scr", [4, 128, KC, NMEM], BF16)
    mvR = [R() for _ in range(4)]
    kTR = [R() for _ in range(4)]
    hpR = [[R() for _ in range(16)] for _ in range(2)]
    gpR = [R() for _ in range(4)]
    stR = R()
    outR = R()

    xT = kb.sb([128, KC, T], F32, KC, "xT")
    uT = kb.sb([128, KC, PF + T], BF16, KC, "uT")
    yT = kb.sb([128, KC, T], F32, KC, "yT")
    hT = kb.sb([128, KC, T], BF16, KC, "hT")
    cst = kb.sb([128, 6, 128], BF16, 1, "cst")
    gains = kb.sb([128, 24, KC], F32, 1, "gains")
    memg = kb.sb([128, 4, KC], F32, 1, "memg")
    lbraw = kb.sb([128, 4, 16], F32, 1, "lbraw")
    lbv = kb.sb([128, 2, 16], F32, 1, "lbv")
    lb0 = kb.sb([128, 2, 16], F32, 1, "lb0")
    hgn = kb.sb([128, 2], F32, 1, "hgn")
    ggn = kb.sb([128, 4], F32, 1, "ggn")
    gbk = kb.sb([128, 8], F32, 1, "gbk")
    ngbk = kb.sb([128, 8], F32, 1, "ngbk")
    psc = kb.sb([128, KC], F32, 1, "psc")
    cm32 = kb.sb([32, 16, 32], F32, 1, "cm32")
    rst = kb.sb([128, T], F32, 1, "rst")
    onesf = kb.sb([128, T], F32, 1, "onesf")
    icnt = kb.sb([128, 4, 16], F32, 1, "icnt")
    dmask = kb.sb([16, 16, 128], BF16, 1, "dmask")
    psel = kb.sb([120, 2, 4, NS], BF16, 1, "psel")
    w1sb = kb.sb([128, KC, 16], BF16, 1, "w1sb")
    w2sb = kb.sb([16, 1024], BF16, 1, "w2sb")
    KV = [kb.sb([128, 4096], BF16, 1, "KV%d" % i) for i in range(2)]
    lbe = kb.sb([128, 4, 16], F32, 1, "lbe")
    lbm = kb.sb([128, 16], F32, 1, "lbm")

    ident = cst.t[:, 0, :]
    onesD = cst.t[:, 1, :]
    ones128 = cst.t[:, 2, :]
    ones512 = cst.t[:, 3, :]
    ones1 = cst.t[:, 4, :]
    tri = cst.t[:, 5, :]

    WC = 256
    wpool = kb.rot([128, KC, WC], BF16, 3, "wb")
    banks = Rot([Buf(nc.alloc_psum_tensor("bank%d" % i, [128, 512], F32)) for i in range(8)])
    sq_p = kb.rot([128, T], BF16, 2, "sq")
    f32_p = kb.rot([128, T], F32, 3, "f32")

    held = set()

    def bank(hold=False):
        while True:
            bk = banks.next()
            if id(bk) not in held:
                break
        if hold:
            held.add(id(bk))
        return bk

    def release(bk):
        held.discard(id(bk))

    t1_p = kb.rot([128, T], F32, 1, "t1")

    dma("pool", [(cst.t[:], cst_d[:, :, :])], writes=[cst.R])
    dma("pool", [(dmask.t[:], dmask_d[:, :, :])], writes=[dmask.R])
    dma("pool", [(psel.t[:], psel_d[:, :, :, :])], writes=[psel.R])
    dma("pool", [(w1sb.t[:], gl_w1[0].rearrange("(k p) n -> p k n", p=128))], writes=[w1sb.R])
    dma("pool", [(w2sb.t[:], gl_w2[0])], writes=[w2sb.R])
    for b_, d_ in ((gains, gains_d), (memg, memg_d), (lbraw, lb_d), (hgn, hgn_d), (ggn, ggn_d), (gbk, gbk_d),
                   (psc, psc_d), (cm32, cm32_d), (rst, rst_d), (icnt, icnt_d)):
        dma("sp", [(b_.t[:], d_)], writes=[b_.R], owner=stR)
    op("dve", lambda e: e.memset(onesf.t[:], 1.0), writes=[onesf.R])
    epsb = kb.sb([128, 1], F32, 1, "epsb")
    op("dve", lambda e: e.memset(epsb.t[:], EPS), writes=[epsb.R])
    op("dve", lambda e: e.memset(lb0.t[:, 0, :], 0.0), writes=[lb0.R])
    op("dve", lambda e: e.memset(lb0.t[:, 1, :], 1.0), writes=[lb0.R])
    op("dve", lambda e: e.tensor_scalar(out=ngbk.t[:], in0=gbk.t[:], scalar1=-1.0, scalar2=None, op0=ALU.mult),
       reads=[gbk.R], writes=[ngbk.R])
    op("dve", lambda e: e.tensor_tensor(out=lbm.t[:], in0=lbraw.t[:, 0, :], in1=lbraw.t[:, 1, :], op=ALU.max),
       reads=[lbraw.R], writes=[lbm.R])
    for i in (2, 3):
        op("dve", lambda e, i=i: e.tensor_tensor(out=lbm.t[:], in0=lbm.t[:], in1=lbraw.t[:, i, :], op=ALU.max),
           reads=[lbraw.R, lbm.R], writes=[lbm.R])
    for i in range(4):
        op("dve", lambda e, i=i: e.tensor_tensor(out=lbe.t[:, i, :], in0=lbraw.t[:, i, :], in1=lbm.t[:],
                                                 op=ALU.subtract), reads=[lbraw.R, lbm.R], writes=[lbe.R])
    op("act", lambda e: e.activation(out=lbe.t[:], in_=lbe.t[:], func=AF.Exp), reads=[lbe.R], writes=[lbe.R])
    op("dve", lambda e: e.tensor_tensor(out=lbm.t[:], in0=lbe.t[:, 1, :], in1=lbe.t[:, 2, :], op=ALU.add),
       reads=[lbe.R], writes=[lbm.R])
    op("dve", lambda e: e.tensor_tensor(out=lbm.t[:], in0=lbm.t[:], in1=lbe.t[:, 3, :], op=ALU.add),
       reads=[lbe.R, lbm.R], writes=[lbm.R])
    op("dve", lambda e: e.tensor_tensor(out=lbv.t[:, 1, :], in0=lbm.t[:], in1=lbe.t[:, 0, :], op=ALU.add),
       reads=[lbe.R, lbm.R], writes=[lbv.R])
    op("dve", lambda e: e.reciprocal(out=lbv.t[:, 1, :], in_=lbv.t[:, 1, :]), reads=[lbv.R], writes=[lbv.R])
    op("dve", lambda e: e.tensor_tensor(out=lbv.t[:, 0, :], in0=lbm.t[:], in1=lbv.t[:, 1, :], op=ALU.mult),
       reads=[lbm.R, lbv.R], writes=[lbv.R])
    op("dve", lambda e: e.tensor_tensor(out=lbv.t[:, 1, :], in0=lbe.t[:, 0, :], in1=lbv.t[:, 1, :], op=ALU.mult),
       reads=[lbe.R, lbv.R], writes=[lbv.R])

    def U(c, n, c0=0):
        return uT.t[:, c, PF + c0:PF + c0 + n]

    def wview(wd, r0, c0, ncols, nk=KC):
        return wd[r0:r0 + nk * 128, c0:c0 + ncols].rearrange("(k p) n -> p k n", p=128)

    def wload(parts):
        wb = wpool.next()
        prs = []
        for v, off in parts:
            nk, ncol = v.shape[1], v.shape[2]
            prs.append((wb.t[:, 0:nk, off:off + ncol], v))
        dma("pool", prs, writes=[wb.R])
        return wb

    def mm(ps, pcols, lhs_list, rhs_list, reads, prow=slice(0, 128)):
        nn = len(lhs_list)

        def fn(e):
            inst = None
            for i in range(nn):
                inst = e.matmul(ps.t[prow, pcols], lhs_list[i], rhs_list[i], start=(i == 0), stop=(i == nn - 1))
            return inst
        op("pe", fn, reads=reads, writes=[ps.R])

    def stats(src, srcR, n, nch, onesm):
        ps = bank()
        for c in range(nch):
            sq = sq_p.next()
            op("act", lambda e, c=c, sq=sq: e.activation(out=sq.t[:, 0:n], in_=src(c), func=AF.Square),
               reads=[srcR(c)], writes=[sq.R])
            op("pe", lambda e, c=c, sq=sq: e.matmul(ps.t[:, 0:n], onesm, sq.t[:, 0:n], start=(c == 0),
                                                    stop=(c == nch - 1)), reads=[sq.R, cst.R], writes=[ps.R])
        rstd = f32_p.next()
        op("act", lambda e: e.activation(out=rstd.t[:, 0:n], in_=ps.t[:, 0:n], func=AF.Sqrt, bias=epsb.t[:, 0:1]),
           reads=[ps.R, epsb.R], writes=[rstd.R])
        op("dve", lambda e: e.reciprocal(out=rstd.t[:, 0:n], in_=rstd.t[:, 0:n]), reads=[rstd.R], writes=[rstd.R])
        return rstd

    def prenorm(n, gi, gt=None):
        gt = gt or gains
        rstd = stats(lambda c: xT.t[:, c, 0:n], lambda c: xT.rs[c], n, KC, onesD)
        for c in range(KC):
            op("dve", lambda e, c=c: e.scalar_tensor_tensor(
                out=U(c, n), in0=xT.t[:, c, 0:n], scalar=gt.t[:, gi, c:c + 1],
                in1=rstd.t[:, 0:n], op0=ALU.mult, op1=ALU.mult),
               reads=[xT.rs[c], rstd.R, gt.R], writes=[uT.rs[c]])

    def postnorm_add(n, gi, gi_next=None):
        rstd = stats(lambda c: yT.t[:, c, 0:n], lambda c: yT.rs[c], n, KC, onesD)
        for c in range(KC):
            op("dve", lambda e, c=c: e.scalar_tensor_tensor(
                out=yT.t[:, c, 0:n], in0=yT.t[:, c, 0:n], scalar=gains.t[:, gi, c:c + 1],
                in1=rstd.t[:, 0:n], op0=ALU.mult, op1=ALU.mult),
               reads=[rstd.R, gains.R], writes=[yT.rs[c]])
        ps = bank() if gi_next is not None else None
        for c in range(KC):
            op("dve", lambda e, c=c: e.tensor_tensor(
                out=xT.t[:, c, 0:n], in0=xT.t[:, c, 0:n], in1=yT.t[:, c, 0:n], op=ALU.add),
               reads=[yT.rs[c]], writes=[xT.rs[c]])
            if gi_next is not None:
                sq = sq_p.next()
                op("act", lambda e, c=c, sq=sq: e.activation(out=sq.t[:, 0:n], in_=xT.t[:, c, 0:n], func=AF.Square),
                   reads=[xT.rs[c]], writes=[sq.R])
                op("pe", lambda e, c=c, sq=sq: e.matmul(ps.t[:, 0:n], onesD, sq.t[:, 0:n], start=(c == 0),
                                                        stop=(c == KC - 1)), reads=[sq.R, cst.R], writes=[ps.R])
        if gi_next is not None:
            rstd2 = f32_p.next()
            op("act", lambda e: e.activation(out=rstd2.t[:, 0:n], in_=ps.t[:, 0:n], func=AF.Sqrt, bias=epsb.t[:, 0:1]),
               reads=[ps.R, epsb.R], writes=[rstd2.R])
            op("dve", lambda e: e.reciprocal(out=rstd2.t[:, 0:n], in_=rstd2.t[:, 0:n]), reads=[rstd2.R], writes=[rstd2.R])
            for c in range(KC):
                op("dve", lambda e, c=c: e.scalar_tensor_tensor(
                    out=U(c, n), in0=xT.t[:, c, 0:n], scalar=gains.t[:, gi_next, c:c + 1],
                    in1=rstd2.t[:, 0:n], op0=ALU.mult, op1=ALU.mult),
                   reads=[xT.rs[c], rstd2.R, gains.R], writes=[uT.rs[c]])

    def linear(n, src, srcap, wd, r0, c0w, nout_chunks, evac, nk=KC, src_k0=0):
        per = WC // 128
        for mb in range(0, nout_chunks, per):
            nm = min(per, nout_chunks - mb)
            wb = wload([(wview(wd, r0, c0w + mb * 128, nm * 128, nk), 0)])
            for mi in range(nm):
                m = mb + mi
                ps = bank()
                mm(ps, slice(0, n), [wb.t[:, k, mi * 128:(mi + 1) * 128] for k in range(nk)],
                   [srcap(src_k0 + k) for k in range(nk)], [wb.R] + [src.rs[src_k0 + k] for k in range(nk)])
                evac(m, ps)

    def evac_copy(dst, n, scale=None):
        def f(m, ps):
            if scale is None:
                op("act", lambda e: e.activation(out=dst.t[:, m, 0:n], in_=ps.t[:, 0:n], func=AF.Copy),
                   reads=[ps.R], writes=[dst.rs[m]])
            else:
                op("dve", lambda e: e.tensor_scalar(out=dst.t[:, m, 0:n], in0=ps.t[:, 0:n], scalar1=scale(m),
                                                    scalar2=None, op0=ALU.mult), reads=[ps.R], writes=[dst.rs[m]])
        return f

    def gated_norm(o_ps, n, gn_ap, gate, onesm, dst_chunk0):
        nch = len(o_ps)
        rstd = stats(lambda c: o_ps[c].t[:, 0:n], lambda c: o_ps[c].R, n, nch, onesm)
        for c in range(nch):
            t1 = t1_p.next()
            op("dve", lambda e, c=c, t1=t1: e.scalar_tensor_tensor(
                out=t1.t[:, 0:n], in0=o_ps[c].t[:, 0:n], scalar=gn_ap(c), in1=rstd.t[:, 0:n],
                op0=ALU.mult, op1=ALU.mult), reads=[o_ps[c].R, rstd.R], writes=[t1.R])
            op("dve", lambda e, c=c, t1=t1: e.tensor_tensor(
                out=hT.t[:, dst_chunk0 + c, 0:n], in0=t1.t[:, 0:n], in1=gate[c][0], op=ALU.mult),
               reads=[t1.R, gate[c][1]], writes=[hT.rs[dst_chunk0 + c]])

    class View:
        def __init__(self, t):
            self.t = t
            self.R = R()

    def hgrn(n, mode, tile, li, j):
        lbb = lb0 if li == 0 else lbv
        w_in = hg_w_in[j]
        if mode == "p":
            fr = Rot([View(yT.t[:, 0:4, :]), View(yT.t[:, 4:8, :])])
            tmpf_r = Rot([[View(yT.t[:, 8, :]), View(yT.t[:, 9, :])], [View(yT.t[:, 10, :]), View(yT.t[:, 11, :])]])
            Sall = View(yT.t[:, 12:16, :].rearrange("p c (a d) -> p (c a) d", d=128))
            vTr = kb.phrot([128, n], BF16, 2, "hvT")
            bf_r = Rot([[kb.ph([128, T], BF16, 1, "hbf") for _ in range(3)] for _ in range(2)])
            Sb_r = kb.phrot([128, 16, 128], BF16, 2, "Sb")
            vtok_r = kb.phrot([32, 16, 128], BF16, 2, "vtok")
            khtok_r = kb.phrot([32, 16, 128], BF16, 2, "khtok")
            P_r = kb.phrot([32, 16, 32], BF16, 2, "P")
            dj_r = kb.phrot([128, 16], F32, 2, "dj")
            Sfin = kb.phrot([128, 128], F32, 2, "Sfin")
        else:
            fr = kb.phrot([128, 4, n], F32, 1, "hfr")
            vTr = kb.phrot([128, n], BF16, 2, "hvT")
            Sin = kb.phrot([128, NS, 128], F32, 2, "Sin")
            kvb = kb.ph([128, NS], BF16, 1, "kvb")
            tok = kb.ph([16, 256], BF16, 1, "tok")
            mskd = kb.ph([16, NS, 128], BF16, 1, "mskd")

        def stageA(h):
            wb = wload([(wview(w_in, 0, p * 2048 + h * 128, 128), p * 128) for p in range(2)])
            wb2 = wload([(wview(w_in, 0, (2 + p) * 2048 + h * 128, 128), p * 128) for p in range(2)])
            pss = []
            for p in range(4):
                ps = bank()
                wbb = wb if p < 2 else wb2
                mm(ps, slice(0, n), [wbb.t[:, k, (p % 2) * 128:(p % 2 + 1) * 128] for k in range(KC)],
                   [U(k, n) for k in range(KC)], [wbb.R] + uT.rs)
                pss.append(ps)
            fb = fr.next()
            qs, f, gate, kk = (fb.t[:, i, :] for i in range(4))
            vT = vTr.next()
            op("act", lambda e: e.activation(out=qs, in_=pss[0].t[:, 0:n], func=AF.Silu), reads=[pss[0].R], writes=[fb.R])
            op("act", lambda e: e.activation(out=f, in_=pss[1].t[:, 0:n], func=AF.Sigmoid), reads=[pss[1].R], writes=[fb.R])
            op("act", lambda e: e.activation(out=vT.t[:, 0:n], in_=pss[2].t[:, 0:n], func=AF.Copy), reads=[pss[2].R], writes=[vT.R])
            op("act", lambda e: e.activation(out=gate, in_=pss[3].t[:, 0:n], func=AF.Silu), reads=[pss[3].R], writes=[fb.R])
            op("dve", lambda e: e.tensor_scalar(out=f, in0=f, scalar1=lbb.t[:, 1, h:h + 1], scalar2=lbb.t[:, 0, h:h + 1],
                                                op0=ALU.mult, op1=ALU.add), reads=[lbb.R], writes=[fb.R])
            op("dve", lambda e: e.tensor_scalar(out=kk, in0=f, scalar1=-1.0, scalar2=1.0, op0=ALU.mult, op1=ALU.add),
               reads=[], writes=[fb.R])
            cx = dict(h=h, fb=fb, qs=qs, f=f, gate=gate, kk=kk, vT=vT)
            return cx

        def stageA2(cx):
            h, fb, qs, f, gate, kk, vT = (cx[k_] for k_ in ("h", "fb", "qs", "f", "gate", "kk", "vT"))
            if mode == "p":
                logf, et = tmpf_r.next()
                b = logf
                qt, kt, khT = bf_r.next()
                dj = dj_r.next()
                op("act", lambda e: e.activation(out=logf.t[:], in_=f, func=AF.Ln), reads=[fb.R], writes=[logf.R])
                op("dve", lambda e: e.tensor_tensor_scan(out=b.t[:], data0=rst.t[:], data1=logf.t[:], initial=0.0,
                                                         op0=ALU.mult, op1=ALU.add), reads=[logf.R, rst.R], writes=[b.R])
                op("act", lambda e: e.activation(out=et.t[:], in_=b.t[:], func=AF.Exp), reads=[b.R], writes=[et.R])
                op("dve", lambda e: e.tensor_tensor(out=qt.t[:], in0=qs, in1=et.t[:], op=ALU.mult),
                   reads=[fb.R, et.R], writes=[qt.R])
                op("act", lambda e: e.activation(out=et.t[:], in_=b.t[:], func=AF.Exp, scale=-1.0), reads=[b.R], writes=[et.R])
                op("dve", lambda e: e.tensor_tensor(out=kt.t[:], in0=kk, in1=et.t[:], op=ALU.mult),
                   reads=[fb.R, et.R], writes=[kt.R])
                b3 = b.t[:].rearrange("p (j t) -> p j t", t=32)
                op("dve", lambda e: e.tensor_tensor(out=et.t[:].rearrange("p (j t) -> p j t", t=32),
                                                    in0=b3[:, :, 31:32].broadcast_to([128, 16, 32]), in1=b3,
                                                    op=ALU.subtract), reads=[b.R], writes=[et.R])
                op("act", lambda e: e.activation(out=et.t[:], in_=et.t[:], func=AF.Exp), reads=[et.R], writes=[et.R])
                op("dve", lambda e: e.tensor_tensor(out=khT.t[:], in0=kk, in1=et.t[:], op=ALU.mult),
                   reads=[fb.R, et.R], writes=[khT.R])
                op("act", lambda e: e.activation(out=dj.t[:], in_=b3[:, :, 31], func=AF.Exp), reads=[b.R], writes=[dj.R])
                cx.update(qt=qt, kt=kt, khT=khT, dj=dj)

        def stageB(cx):
            h, fb, qs, f, gate, kk, vT = (cx[k_] for k_ in ("h", "fb", "qs", "f", "gate", "kk", "vT"))
            if mode == "p":
                qt, kt, khT, dj = cx["qt"], cx["kt"], cx["khT"], cx["dj"]
                vtok, khtok, P, Sb = vtok_r.next(), khtok_r.next(), P_r.next(), Sb_r.next()
                for (srcb, dstb) in ((vT, vtok), (khT, khtok)):
                    for half in range(2):
                        ps = bank()
                        psb = ps.t[:].bitcast(BF16)

                        def fn(e, srcb=srcb, half=half, psb=psb):
                            inst = None
                            for jj in range(8):
                                jc = half * 8 + jj
                                inst = e.transpose(out=psb[0:32, jj * 128:(jj + 1) * 128],
                                                   in_=srcb.t[:, jc * 32:(jc + 1) * 32], identity=ident)
                            return inst
                        op("pe", fn, reads=[srcb.R, cst.R], writes=[ps.R])
                        op("act", lambda e, dstb=dstb, half=half, psb=psb: e.activation(
                            out=dstb.t[:, half * 8:(half + 1) * 8, :],
                            in_=psb[0:32, 0:1024].rearrange("p (j d) -> p j d", d=128), func=AF.Copy),
                           reads=[ps.R], writes=[dstb.R])
                ps_s = bank()

                def fn_s(e, ps_s=ps_s):
                    inst = None
                    for jc in range(16):
                        inst = e.matmul(ps_s.t[0:32, jc * 32:(jc + 1) * 32], kt.t[:, jc * 32:(jc + 1) * 32],
                                        qt.t[:, jc * 32:(jc + 1) * 32], start=True, stop=True)
                    return inst
                op("pe", fn_s, reads=[kt.R, qt.R], writes=[ps_s.R])
                op("dve", lambda e, ps_s=ps_s: e.tensor_tensor(
                    out=P.t[:], in0=ps_s.t[0:32, :].rearrange("p (j t) -> p j t", t=32), in1=cm32.t[:], op=ALU.mult),
                   reads=[ps_s.R, cm32.R], writes=[P.R])
                if tile == 0:
                    op("dve", lambda e: e.memset(Sall.t[:, 0, :], 0.0), writes=[Sall.R])
                else:
                    dma("sp", [(Sall.t[:, 0, :], hp_out[j, h, :, :])], reads=[hpR[j][h]], writes=[Sall.R], owner=stR)
                Sf = Sfin.next()
                for q4 in range(4):
                    psU = bank()

                    def fn_u(e, q4=q4, psU=psU):
                        inst = None
                        for jj in range(4):
                            jc = q4 * 4 + jj
                            inst = e.matmul(psU.t[:, jj * 128:(jj + 1) * 128], khtok.t[:, jc, :], vtok.t[:, jc, :],
                                            start=True, stop=True)
                        return inst
                    op("pe", fn_u, reads=[khtok.R, vtok.R], writes=[psU.R])
                    for jj in range(4):
                        jc = q4 * 4 + jj
                        if jc < 15:
                            op("dve", lambda e, jc=jc, jj=jj, psU=psU: e.scalar_tensor_tensor(
                                out=Sall.t[:, jc + 1, :], in0=Sall.t[:, jc, :], scalar=dj.t[:, jc:jc + 1],
                                in1=psU.t[:, jj * 128:(jj + 1) * 128], op0=ALU.mult, op1=ALU.add),
                               reads=[psU.R, dj.R, Sall.R], writes=[Sall.R])
                        else:
                            op("dve", lambda e, jc=jc, jj=jj, psU=psU: e.scalar_tensor_tensor(
                                out=Sf.t[:], in0=Sall.t[:, jc, :], scalar=dj.t[:, jc:jc + 1],
                                in1=psU.t[:, jj * 128:(jj + 1) * 128], op0=ALU.mult, op1=ALU.add),
                               reads=[psU.R, dj.R, Sall.R], writes=[Sf.R])
                dma("sp", [(hp_out[j, h, :, :], Sf.t[:])], reads=[Sf.R], writes=[hpR[j][h]], owner=outR)
                op("dve", lambda e: e.tensor_copy(out=Sb.t[:], in_=Sall.t[:]), reads=[Sall.R], writes=[Sb.R])
                cx.update(vtok=vtok, P=P, Sb=Sb)
                return
            else:
                Si = Sin.next()
                dma("sp", [(Si.t[:], st_h[j, :, h, :, :].rearrange("s k d -> k s d"))], writes=[Si.R], owner=stR)
                op("act", lambda e: e.activation(out=kvb.t[:], in_=kk, func=AF.Copy), reads=[fb.R], writes=[kvb.R])
                ps = bank()
                psb = ps.t[:].bitcast(BF16)

                def fn(e, psb=psb):
                    e.transpose(out=psb[0:16, 0:128], in_=kvb.t[:], identity=ident)
                    return e.transpose(out=psb[0:16, 128:256], in_=vT.t[:, 0:n], identity=ident)
                op("pe", fn, reads=[kvb.R, vT.R, cst.R], writes=[ps.R])
                op("act", lambda e, psb=psb: e.activation(out=tok.t[:], in_=psb[0:16, 0:256], func=AF.Copy),
                   reads=[ps.R], writes=[tok.R])
                op("dve", lambda e: e.tensor_tensor(out=mskd.t[:], in0=tok.t[:, 128:256].unsqueeze(1).broadcast_to([16, NS, 128]),
                                                    in1=dmask.t[:], op=ALU.mult), reads=[tok.R, dmask.R], writes=[mskd.R])
                for q4 in range(4):
                    psU = bank()

                    def fn_u(e, q4=q4, psU=psU):
                        inst = None
                        for jj in range(4):
                            s = q4 * 4 + jj
                            inst = e.matmul(psU.t[:, jj * 128:(jj + 1) * 128], tok.t[:, 0:128], mskd.t[:, s, :],
                                            start=True, stop=True)
                        return inst
                    op("pe", fn_u, reads=[tok.R, mskd.R], writes=[psU.R])
                    for jj in range(4):
                        s = q4 * 4 + jj
                        op("dve", lambda e, s=s, jj=jj, psU=psU: e.scalar_tensor_tensor(
                            out=Si.t[:, s, :], in0=Si.t[:, s, :], scalar=fb.t[:, 1, s:s + 1],
                            in1=psU.t[:, jj * 128:(jj + 1) * 128], op0=ALU.mult, op1=ALU.add),
                           reads=[psU.R, fb.R], writes=[Si.R])
                ps_o = bank()

                def fn_o(e, ps_o=ps_o, Si=Si):
                    inst = None
                    for s in range(NS):
                        inst = e.matmul(ps_o.t[:, s:s + 1], Si.t[:, s, :], fb.t[:, 0, s:s + 1], start=True, stop=True)
                    return inst
                op("pe", fn_o, reads=[Si.R, fb.R], writes=[ps_o.R])
                dma("sp", [(hs_out[j, :, h, :, :].rearrange("s k d -> k s d"), Si.t[:])], reads=[Si.R], owner=outR)
            gated_norm([ps_o], n, lambda c: hgn.t[:, j:j + 1], [(gate, fb.R)], ones128, h)

        def stageB2(cx):
            h, fb, gate, qt = cx["h"], cx["fb"], cx["gate"], cx["qt"]
            vtok, P, Sb = cx["vtok"], cx["P"], cx["Sb"]
            ps_o = bank()

            def fn_o(e, ps_o=ps_o):
                inst = None
                for jc in range(16):
                    cs = slice(jc * 32, (jc + 1) * 32)
                    e.matmul(ps_o.t[:, cs], vtok.t[:, jc, :], P.t[:, jc, :], start=True, stop=False)
                    inst = e.matmul(ps_o.t[:, cs], Sb.t[:, jc, :], qt.t[:, cs], start=False, stop=True)
                return inst
            op("pe", fn_o, reads=[vtok.R, P.R, Sb.R, qt.R], writes=[ps_o.R])
            gated_norm([ps_o], n, lambda c: hgn.t[:, j:j + 1], [(gate, fb.R)], ones128, h)

        if mode == "p":
            cxs = {}
            for i in range(18):
                if 0 <= i - 2 < 16:
                    stageB2(cxs.pop(i - 2))
                if i < 16:
                    cxs[i] = stageA(i)
                if 0 <= i - 1 < 16:
                    stageB(cxs[i - 1])
                if i < 16:
                    stageA2(cxs[i])
        else:
            for h in range(16):
                cx = stageA(h)
                stageA2(cx)
                stageB(cx)

    def gla(n, mode, tile):
        w_in = gl_w_in[0]
        rb = kb.ph([16, T], BF16, 1, "glr")
        ps = bank()
        mm(ps, slice(0, n), [w1sb.t[:, k, :] for k in range(KC)], [U(k, n) for k in range(KC)], [w1sb.R] + uT.rs,
           prow=slice(0, 16))
        op("act", lambda e: e.activation(out=rb.t[:, 0:n], in_=ps.t[0:16, 0:n], func=AF.Copy), reads=[ps.R], writes=[rb.R])
        qk = kb.ph([128, 4, n], F32, 1, "gqk")
        lg = kb.ph([128, 2, n], F32, 1, "glg")
        gate = kb.ph([128, 4, n], BF16, 1, "ggate")
        if mode == "p":
            et = kb.ph([128, 2, T], F32, 1, "get")
            qt = kb.ph([128, 2, T], BF16, 1, "gqt")
            kt = kb.ph([128, 2, T], BF16, 1, "gkt")
            khT = kb.ph([128, 2, T], BF16, 1, "gkhT")
            vtok = kb.ph([128, 4, 512], BF16, 1, "gvtok")
            khtok = kb.ph([128, 4, 256], BF16, 1, "gkhtok")
            PT = kb.ph([128, 4, T], BF16, 1, "gPT")
            dend = kb.ph([128, 2], F32, 1, "gdend")
            Sg = kb.ph([128, 2, 512], F32, 1, "gS")
            Sgb = kb.ph([128, 2, 512], BF16, 1, "gSb")
        else:
            vTs = kb.ph([128, 4, NS], BF16, 1, "gvTs")
            kvb = kb.ph([128, 2, NS], BF16, 1, "gkvb")
            tokk = kb.ph([16, 256], BF16, 1, "gtokk")
            tokv = kb.ph([16, 512], BF16, 1, "gtokv")
            mskd = kb.ph([16, 4, 512], BF16, 1, "gmskd")
            Sin = kb.phrot([128, 4, 2, 512], F32, 2, "gSin")
        for h in range(4):
            for part in range(2):
                wb = wload([(wview(w_in, 0, part * 1024 + h * 256, 256), 0)])
                for mi in range(2):
                    ps = bank()
                    mm(ps, slice(0, n), [wb.t[:, k, mi * 128:(mi + 1) * 128] for k in range(KC)],
                       [U(k, n) for k in range(KC)], [wb.R] + uT.rs)
                    sc = (256.0 ** -0.5) if part == 0 else 1.0
                    op("act", lambda e, ps=ps, part=part, mi=mi, sc=sc: e.activation(
                        out=qk.t[:, part * 2 + mi, :], in_=ps.t[:, 0:n], func=AF.Copy, scale=sc),
                       reads=[ps.R], writes=[qk.R])
            for mi in range(2):
                ps = bank()
                mm(ps, slice(0, n), [w2sb.t[:, (h * 2 + mi) * 128:(h * 2 + mi + 1) * 128]], [rb.t[:, 0:n]],
                   [w2sb.R, rb.R])
                op("act", lambda e, ps=ps, mi=mi: e.activation(out=lg.t[:, mi, :], in_=ps.t[:, 0:n], func=AF.Exp,
                                                               scale=-1.0, bias=ngbk.t[:, h * 2 + mi:h * 2 + mi + 1]),
                   reads=[ps.R, ngbk.R], writes=[lg.R])
            op("act", lambda e: e.activation(out=lg.t[:], in_=lg.t[:], func=AF.Ln, bias=1.0), reads=[lg.R], writes=[lg.R])
            op("dve", lambda e: e.tensor_scalar(out=lg.t[:], in0=lg.t[:], scalar1=-1.0 / 16.0, scalar2=None, op0=ALU.mult),
               reads=[lg.R], writes=[lg.R])
            for cb in range(2):
                wb = wload([(wview(w_in, 0, 4096 + h * 512 + cb * 256, 256), 0)])
                for mi in range(2):
                    ps = bank()
                    mm(ps, slice(0, n), [wb.t[:, k, mi * 128:(mi + 1) * 128] for k in range(KC)],
                       [U(k, n) for k in range(KC)], [wb.R] + uT.rs)
                    op("act", lambda e, ps=ps, c=cb * 2 + mi: e.activation(out=gate.t[:, c, :], in_=ps.t[:, 0:n],
                                                                           func=AF.Silu), reads=[ps.R], writes=[gate.R])
            if mode == "p":
                for cb in range(2):
                    wb = wload([(wview(w_in, 0, 2048 + h * 512 + cb * 256, 256), 0)])
                    for tb in range(4):
                        ps = bank()
                        mm(ps, slice(0, 256), [U(k, 128, tb * 128) for k in range(KC)],
                           [wb.t[:, k, :] for k in range(KC)], [wb.R] + uT.rs)
                        op("act", lambda e, ps=ps, tb=tb, cb=cb: e.activation(
                            out=vtok.t[:, tb, cb * 256:(cb + 1) * 256], in_=ps.t[:, 0:256], func=AF.Copy),
                           reads=[ps.R], writes=[vtok.R])
                for kc in range(2):
                    op("dve", lambda e, kc=kc: e.tensor_tensor_scan(out=lg.t[:, kc, :], data0=onesf.t[:], data1=lg.t[:, kc, :],
                                                                    initial=0.0, op0=ALU.mult, op1=ALU.add),
                       reads=[lg.R, onesf.R], writes=[lg.R])
                op("act", lambda e: e.activation(out=et.t[:], in_=lg.t[:], func=AF.Exp), reads=[lg.R], writes=[et.R])
                op("dve", lambda e: e.tensor_tensor(out=qt.t[:], in0=qk.t[:, 0:2, :], in1=et.t[:], op=ALU.mult),
                   reads=[qk.R, et.R], writes=[qt.R])
                op("act", lambda e: e.activation(out=dend.t[:], in_=lg.t[:, :, T - 1], func=AF.Exp), reads=[lg.R], writes=[dend.R])
                op("act", lambda e: e.activation(out=et.t[:], in_=lg.t[:], func=AF.Exp, scale=-1.0), reads=[lg.R], writes=[et.R])
                op("dve", lambda e: e.tensor_tensor(out=kt.t[:], in0=qk.t[:, 2:4, :], in1=et.t[:], op=ALU.mult),
                   reads=[qk.R, et.R], writes=[kt.R])
                op("dve", lambda e: e.tensor_tensor(out=et.t[:], in0=lg.t[:, :, T - 1:T].broadcast_to([128, 2, T]),
                                                    in1=lg.t[:], op=ALU.subtract), reads=[lg.R], writes=[et.R])
                op("act", lambda e: e.activation(out=et.t[:], in_=et.t[:], func=AF.Exp), reads=[et.R], writes=[et.R])
                op("dve", lambda e: e.tensor_tensor(out=khT.t[:], in0=qk.t[:, 2:4, :], in1=et.t[:], op=ALU.mult),
                   reads=[qk.R, et.R], writes=[khT.R])
                for tb in range(4):
                    ps = bank()
                    psb = ps.t[:].bitcast(BF16)

                    def fn(e, tb=tb, psb=psb):
                        e.transpose(out=psb[:, 0:128], in_=khT.t[:, 0, tb * 128:(tb + 1) * 128], identity=ident)
                        return e.transpose(out=psb[:, 128:256], in_=khT.t[:, 1, tb * 128:(tb + 1) * 128], identity=ident)
                    op("pe", fn, reads=[khT.R, cst.R], writes=[ps.R])
                    op("act", lambda e, tb=tb, psb=psb: e.activation(out=khtok.t[:, tb, :], in_=psb[:, 0:256], func=AF.Copy),
                       reads=[ps.R], writes=[khtok.R])
                if tile == 0:
                    op("dve", lambda e: e.memset(Sg.t[:], 0.0), writes=[Sg.R])
                else:
                    dma("sp", [(Sg.t[:], gp_out[0, h].rearrange("(c p) d -> p c d", p=128))], reads=[gpR[h]],
                        writes=[Sg.R], owner=stR)
                op("act", lambda e: e.activation(out=Sgb.t[:], in_=Sg.t[:], func=AF.Copy), reads=[Sg.R], writes=[Sgb.R])
                for i in range(4):
                    ps = bank()
                    cs = slice(i * 128, T)
                    mm(ps, cs, [kt.t[:, kc, i * 128:(i + 1) * 128] for kc in range(2)], [qt.t[:, kc, cs] for kc in range(2)],
                       [kt.R, qt.R])
                    op("dve", lambda e, i=i, ps=ps: e.tensor_tensor(out=PT.t[:, i, i * 128:(i + 1) * 128],
                                                                    in0=ps.t[:, i * 128:(i + 1) * 128], in1=tri, op=ALU.mult),
                       reads=[ps.R, cst.R], writes=[PT.R])
                    if i < 3:
                        op("act", lambda e, i=i, ps=ps: e.activation(out=PT.t[:, i, (i + 1) * 128:T],
                                                                     in_=ps.t[:, (i + 1) * 128:T], func=AF.Copy),
                           reads=[ps.R], writes=[PT.R])
                o_ps = []
                for c in range(4):
                    ps = bank()

                    def fn_o(e, c=c, ps=ps):
                        e.matmul(ps.t[:, 0:T], Sgb.t[:, 0, c * 128:(c + 1) * 128], qt.t[:, 0, :], start=True, stop=False)
                        e.matmul(ps.t[:, 0:T], Sgb.t[:, 1, c * 128:(c + 1) * 128], qt.t[:, 1, :], start=False, stop=False)
                        inst = None
                        for i in range(4):
                            inst = e.matmul(ps.t[:, i * 128:T], vtok.t[:, i, c * 128:(c + 1) * 128], PT.t[:, i, i * 128:T],
                                            start=False, stop=(i == 3))
                        return inst
                    op("pe", fn_o, reads=[Sgb.R, qt.R, vtok.R, PT.R], writes=[ps.R])
                    o_ps.append(ps)
                for kc in range(2):
                    ps = bank()
                    mm(ps, slice(0, 512), [khtok.t[:, i, kc * 128:(kc + 1) * 128] for i in range(4)],
                       [vtok.t[:, i, :] for i in range(4)], [khtok.R, vtok.R])
                    op("dve", lambda e, kc=kc, ps=ps: e.scalar_tensor_tensor(
                        out=Sg.t[:, kc, :], in0=Sg.t[:, kc, :], scalar=dend.t[:, kc:kc + 1], in1=ps.t[:, 0:512],
                        op0=ALU.mult, op1=ALU.add), reads=[ps.R, dend.R], writes=[Sg.R])
                dma("sp", [(gp_out[0, h].rearrange("(c p) d -> p c d", p=128), Sg.t[:])], reads=[Sg.R], writes=[gpR[h]],
                    owner=outR)
            else:
                for cb in range(2):
                    wb = wload([(wview(w_in, 0, 2048 + h * 512 + cb * 256, 256), 0)])
                    for mi in range(2):
                        ps = bank()
                        mm(ps, slice(0, n), [wb.t[:, k, mi * 128:(mi + 1) * 128] for k in range(KC)],
                           [U(k, n) for k in range(KC)], [wb.R] + uT.rs)
                        op("act", lambda e, ps=ps, c=cb * 2 + mi: e.activation(out=vTs.t[:, c, :], in_=ps.t[:, 0:n],
                                                                               func=AF.Copy), reads=[ps.R], writes=[vTs.R])
                op("act", lambda e: e.activation(out=lg.t[:], in_=lg.t[:], func=AF.Exp), reads=[lg.R], writes=[lg.R])
                op("act", lambda e: e.activation(out=kvb.t[:], in_=qk.t[:, 2:4, :], func=AF.Copy), reads=[qk.R], writes=[kvb.R])
                ps = bank()
                psb = ps.t[:].bitcast(BF16)

                def fn(e, psb=psb):
                    inst = None
                    for kc in range(2):
                        inst = e.transpose(out=psb[0:16, kc * 128:(kc + 1) * 128], in_=kvb.t[:, kc, :], identity=ident)
                    for c in range(4):
                        inst = e.transpose(out=psb[0:16, 256 + c * 128:256 + (c + 1) * 128], in_=vTs.t[:, c, :], identity=ident)
                    return inst
                op("pe", fn, reads=[kvb.R, vTs.R, cst.R], writes=[ps.R])
                op("act", lambda e, psb=psb: e.activation(out=tokk.t[:], in_=psb[0:16, 0:256], func=AF.Copy),
                   reads=[ps.R], writes=[tokk.R])
                op("act", lambda e, psb=psb: e.activation(out=tokv.t[:], in_=psb[0:16, 256:768], func=AF.Copy),
                   reads=[ps.R], writes=[tokv.R])
                o_ps = [bank(hold=True) for _ in range(4)]
                for sg_ in range(4):
                    Si = Sin.next()
                    dma("sp", [(Si.t[:, s4, :, :], st_g[0, sg_ * 4 + s4, h].rearrange("(c p) d -> p c d", p=128))
                               for s4 in range(4)], writes=[Si.R], owner=stR)
                    op("dve", lambda e, sg_=sg_: e.tensor_tensor(
                        out=mskd.t[:], in0=tokv.t[:].unsqueeze(1).broadcast_to([16, 4, 512]),
                        in1=dmask.t[:, sg_ * 4:(sg_ + 1) * 4, 0:1].broadcast_to([16, 4, 512]), op=ALU.mult),
                       reads=[tokv.R, dmask.R], writes=[mskd.R])
                    for s4 in range(4):
                        s = sg_ * 4 + s4
                        for kc in range(2):
                            ps = bank()
                            mm(ps, slice(0, 512), [tokk.t[:, kc * 128:(kc + 1) * 128]], [mskd.t[:, s4, :]], [tokk.R, mskd.R])
                            op("dve", lambda e, s=s, s4=s4, kc=kc, ps=ps, Si=Si: e.scalar_tensor_tensor(
                                out=Si.t[:, s4, kc, :], in0=Si.t[:, s4, kc, :], scalar=lg.t[:, kc, s:s + 1], in1=ps.t[:, 0:512],
                                op0=ALU.mult, op1=ALU.add), reads=[ps.R, lg.R], writes=[Si.R])
                        for c in range(4):
                            def fn_o(e, c=c, s=s, s4=s4, Si=Si):
                                e.matmul(o_ps[c].t[:, s:s + 1], Si.t[:, s4, 0, c * 128:(c + 1) * 128], qk.t[:, 0, s:s + 1],
                                         start=True, stop=False)
                                return e.matmul(o_ps[c].t[:, s:s + 1], Si.t[:, s4, 1, c * 128:(c + 1) * 128],
                                                qk.t[:, 1, s:s + 1], start=False, stop=True)
                            op("pe", fn_o, reads=[Si.R, qk.R], writes=[o_ps[c].R])
                    dma("sp", [(gs_out[0, sg_ * 4 + s4, h].rearrange("(c p) d -> p c d", p=128), Si.t[:, s4, :, :])
                               for s4 in range(4)], reads=[Si.R], owner=outR)
            gated_norm(o_ps, n, lambda c: ggn.t[:, c:c + 1], [(gate.t[:, c, :], gate.R) for c in range(4)], ones512, h * 4)
            for bk in o_ps:
                release(bk)

    def pool(n, mode, tile):
        wins = (2, 4, 8, 16)
        if mode == "p":
            A = kb.ph([128, 4, PF + T], F32, 1, "plA")
            B = kb.ph([128, 4, PF + T], F32, 1, "plB")
            if tile == 0:
                for c in range(KC):
                    op("dve", lambda e, c=c: e.memset(uT.t[:, c, 0:PF], 0.0), writes=[uT.rs[c]])
            for g in range(4):
                cs = slice(4 * g, 4 * g + 4)
                rr = [uT.rs[c] for c in range(4 * g, 4 * g + 4)]
                E = PF + T
                op("dve", lambda e, cs=cs: e.tensor_tensor(out=A.t[:, :, 1:E], in0=uT.t[:, cs, 1:E], in1=uT.t[:, cs, 0:E - 1],
                                                           op=ALU.add), reads=rr, writes=[A.R])
                cur, oth = A, B
                sh = 2
                for lvl in range(g):
                    lo = 2 * sh - 1
                    op("dve", lambda e, cur=cur, oth=oth, sh=sh, lo=lo: e.tensor_tensor(
                        out=oth.t[:, :, lo:E], in0=cur.t[:, :, lo:E], in1=cur.t[:, :, lo - sh:E - sh], op=ALU.add),
                       reads=[cur.R], writes=[oth.R])
                    cur, oth = oth, cur
                    sh *= 2
                w = wins[g]
                op("dve", lambda e, cur=cur, cs=cs, w=w: e.scalar_tensor_tensor(
                    out=hT.t[:, cs, 0:T], in0=cur.t[:, :, PF:E], scalar=1.0 / w, in1=uT.t[:, cs, PF:E],
                    op0=ALU.mult, op1=ALU.subtract), reads=[cur.R] + rr, writes=[hT.rs[c] for c in range(4 * g, 4 * g + 4)])
                if tile == 0:
                    t2 = f32_p.next()
                    t2v = t2.t[:, 0:64].rearrange("p (c t) -> p c t", t=16)
                    op("dve", lambda e, cur=cur, g=g, t2v=t2v: e.tensor_tensor(
                        out=t2v, in0=cur.t[:, :, PF:PF + 16], in1=icnt.t[:, g:g + 1, :].broadcast_to([128, 4, 16]),
                        op=ALU.mult), reads=[cur.R, icnt.R], writes=[t2.R])
                    op("dve", lambda e, cs=cs, t2v=t2v: e.tensor_tensor(
                        out=hT.t[:, cs, 0:16], in0=t2v, in1=uT.t[:, cs, PF:PF + 16], op=ALU.subtract),
                       reads=[t2.R] + rr, writes=[hT.rs[c] for c in range(4 * g, 4 * g + 4)])
            for c in range(KC):
                op("act", lambda e, c=c: e.activation(out=uT.t[:, c, 0:PF], in_=uT.t[:, c, T:T + PF], func=AF.Copy),
                   reads=[], writes=[uT.rs[c]])
        else:
            buf = KV[0]
            bv = buf.t[0:120, :].rearrange("p (a d) -> p a d", d=2048)
            dma("pool", [(bv[:, a, :], st_p[0].rearrange("b r d -> (b r) d")[a * 120:(a + 1) * 120, :]) for a in range(2)],
                writes=[buf.R])
            for c in range(KC):
                g = c // 4
                ps = bank()
                mm(ps, slice(0, NS), [bv[:, a, c * 128:(c + 1) * 128] for a in range(2)],
                   [psel.t[:, a, g, :] for a in range(2)], [buf.R, psel.R])
                op("dve", lambda e, c=c, g=g, ps=ps: e.scalar_tensor_tensor(
                    out=hT.t[:, c, 0:NS], in0=U(c, NS), scalar=(1.0 / wins[g] - 1.0), in1=ps.t[:, 0:NS],
                    op0=ALU.mult, op1=ALU.add), reads=[ps.R, uT.rs[c]], writes=[hT.rs[c]])
            dma("sp", [(ps_shift[:, :, :], st_p[0, :, 1:15, :])], owner=outR, reads=[], writes=[R()])
        for g in range(4):
            def ev(m, ps, g=g):
                c = 4 * g + m
                op("dve", lambda e: e.tensor_scalar(out=yT.t[:, c, 0:n], in0=ps.t[:, 0:n], scalar1=psc.t[:, c:c + 1],
                                                    scalar2=None, op0=ALU.mult), reads=[ps.R, psc.R], writes=[yT.rs[c]])
            linear(n, hT, lambda k: hT.t[:, k, 0:n], pl_w[0, g], 0, 0, 4, ev, nk=4, src_k0=4 * g)

    def pool_state_out(n, mode, tile, gi):
        if mode == "p" and tile != NT - 1:
            return
        ncol = PF if mode == "p" else NS
        c0 = T - PF if mode == "p" else 0
        rstd = stats(lambda c: xT.t[:, c, 0:n], lambda c: xT.rs[c], n, KC, onesD)
        stg = kb.ph([128, KC, 16], F32, 1, "pstg")
        for c in range(KC):
            op("dve", lambda e, c=c: e.scalar_tensor_tensor(
                out=stg.t[:, c, 0:ncol], in0=xT.t[:, c, c0:c0 + ncol], scalar=gains.t[:, gi, c:c + 1],
                in1=rstd.t[:, c0:c0 + ncol], op0=ALU.mult, op1=ALU.mult), reads=[xT.rs[c], rstd.R, gains.R], writes=[stg.R])
        dst = pp_out if mode == "p" else ps_new
        dma("sp", [(dst[:, :, :], stg.t[:, :, 0:ncol])], reads=[stg.R], owner=outR)

    def xattn(n, mode, li):
        if mode == "p":
            KT = KV[0].t[:].rearrange("p (c m) -> p c m", m=NMEM)
            VT = KV[1].t[:].rearrange("p (a d) -> p a d", d=2048)
            dma("pool", [(KT, kT_scr[li])], reads=[kTR[li]], writes=[KV[0].R])
            dma("pool", [(VT, mv_out[li].rearrange("(a p) d -> p a d", p=128))], reads=[mvR[li]], writes=[KV[1].R])
            qh = kb.phrot([128, 4, T], BF16, 2, "xqh")
            eT = kb.phrot([128, 2, T], BF16, 2, "xeT")
            def xA(h):
                q_ = qh.next()

                def evq(m, ps, q_=q_):
                    op("act", lambda e: e.activation(out=q_.t[:, m, :], in_=ps.t[:, 0:n], func=AF.Copy, scale=512.0 ** -0.5),
                       reads=[ps.R], writes=[q_.R])
                linear(n, uT, lambda k: U(k, n), xa_wq[li], 0, h * 512, 4, evq)
                return q_

            def xB(h, q_):
                e_ = eT.next()
                for mt in range(2):
                    ps = bank()
                    mm(ps, slice(0, n), [KT[:, h * 4 + dc, mt * 128:(mt + 1) * 128] for dc in range(4)],
                       [q_.t[:, dc, :] for dc in range(4)], [KV[0].R, q_.R])
                    op("act", lambda e, ps=ps, mt=mt, e_=e_: e.activation(out=e_.t[:, mt, :], in_=ps.t[:, 0:n], func=AF.Exp),
                       reads=[ps.R], writes=[e_.R])
                psd = bank()
                mm(psd, slice(0, n), [ones1, ones1], [e_.t[:, 0, :], e_.t[:, 1, :]], [cst.R, e_.R])
                rden = f32_p.next()
                op("dve", lambda e, psd=psd, rden=rden: e.reciprocal(out=rden.t[:, 0:n], in_=psd.t[:, 0:n]),
                   reads=[psd.R], writes=[rden.R])
                for dc in range(4):
                    ps = bank()
                    c = h * 4 + dc
                    mm(ps, slice(0, n), [VT[:, mt, c * 128:(c + 1) * 128] for mt in range(2)],
                       [e_.t[:, mt, :] for mt in range(2)], [KV[1].R, e_.R])
                    op("dve", lambda e, ps=ps, c=c, rden=rden: e.tensor_tensor(out=hT.t[:, c, 0:n], in0=ps.t[:, 0:n],
                                                                               in1=rden.t[:, 0:n], op=ALU.mult),
                       reads=[ps.R, rden.R], writes=[hT.rs[c]])
            qc = xA(0)
            for h in range(4):
                qn = xA(h + 1) if h < 3 else None
                xB(h, qc)
                qc = qn
        else:
            qtok = kb.ph([16, 2048], BF16, 1, "xqtok")
            for cb in range(8):
                wb = wload([(wview(xa_wq[li], 0, cb * 256, 256), 0)])
                ps = bank()
                mm(ps, slice(0, 256), [U(k, NS) for k in range(KC)], [wb.t[:, k, :] for k in range(KC)], [wb.R] + uT.rs,
                   prow=slice(0, 16))
                op("act", lambda e, ps=ps, cb=cb: e.activation(out=qtok.t[:, cb * 256:(cb + 1) * 256], in_=ps.t[0:16, 0:256],
                                                               func=AF.Copy, scale=512.0 ** -0.5), reads=[ps.R], writes=[qtok.R])
            prod = kb.ph([128, 2, 512], F32, 1, "xprod")
            sc = kb.phrot([128, 4, 2], F32, 2, "xsc")
            eb = kb.phrot([128, 4, 2], BF16, 2, "xeb")
            rd = kb.phrot([128, 4], F32, 2, "xrd")
            ps_o = bank(hold=True)
            KTs = KV[0].t[:].rearrange("p (a d) -> p a d", d=2048)
            VTs = KV[1].t[:].rearrange("p (a d) -> p a d", d=2048)
            for s in range(NS):
                dma("pool", [(KTs, ck[li, s].rearrange("(a p) d -> p a d", p=128))], writes=[KV[0].R])
                dma("pool", [(VTs, cv[li, s].rearrange("(a p) d -> p a d", p=128))], writes=[KV[1].R])
                sc_, eb_, rd_ = sc.next(), eb.next(), rd.next()
                for h in range(4):
                    ps = bank()
                    mm(ps, slice(0, 512), [dmask.t[:, s, :]], [qtok.t[:, h * 512:(h + 1) * 512]], [dmask.R, qtok.R])
                    op("dve", lambda e, ps=ps, h=h: e.tensor_tensor(
                        out=prod.t[:], in0=KTs[:, :, h * 512:(h + 1) * 512],
                        in1=ps.t[:, 0:512].unsqueeze(1).broadcast_to([128, 2, 512]), op=ALU.mult),
                       reads=[ps.R, KV[0].R], writes=[prod.R])
                    op("dve", lambda e, h=h, sc_=sc_: e.tensor_reduce(out=sc_.t[:, h, :], in_=prod.t[:], axis=AX.X, op=ALU.add),
                       reads=[prod.R], writes=[sc_.R])
                op("act", lambda e, sc_=sc_, eb_=eb_: e.activation(out=eb_.t[:], in_=sc_.t[:], func=AF.Exp),
                   reads=[sc_.R], writes=[eb_.R])
                psd = bank()
                mm(psd, slice(0, 8), [ones1], [eb_.t[:].rearrange("p h m -> p (h m)")], [cst.R, eb_.R])
                op("dve", lambda e, psd=psd, rd_=rd_: e.tensor_reduce(
                    out=rd_.t[:], in_=psd.t[:, 0:8].rearrange("p (h m) -> p h m", m=2), axis=AX.X, op=ALU.add),
                   reads=[psd.R], writes=[rd_.R])
                op("dve", lambda e, rd_=rd_: e.reciprocal(out=rd_.t[:], in_=rd_.t[:]), reads=[rd_.R], writes=[rd_.R])

                def fn_o(e, s=s, eb_=eb_):
                    inst = None
                    for c in range(KC):
                        h = c // 4
                        for mt in range(2):
                            inst = e.matmul(ps_o.t[:, c * NS + s:c * NS + s + 1], VTs[:, mt, c * 128:(c + 1) * 128],
                                            eb_.t[:, h, mt:mt + 1], start=(mt == 0), stop=(mt == 1))
                    return inst
                op("pe", fn_o, reads=[KV[1].R, eb_.R], writes=[ps_o.R])
                op("dve", lambda e, s=s, rd_=rd_: e.tensor_tensor(
                    out=hT.t[:, :, s:s + 1].rearrange("p (h c) o -> p h (c o)", h=4),
                    in0=ps_o.t[:, 0:KC * NS].rearrange("p (h c s) -> p h c s", h=4, s=NS)[:, :, :, s],
                    in1=rd_.t[:].unsqueeze(2).broadcast_to([128, 4, 4]), op=ALU.mult),
                   reads=[ps_o.R, rd_.R], writes=hT.rs)
            release(ps_o)
        linear(n, hT, lambda k: hT.t[:, k, 0:n], xa_wo[li], 0, 0, 16, evac_copy(yT, n))

    def mlp(n, li):
        t1r = kb.phrot([128, T], F32, 2, "mt1")
        for g in range(4):
            def ev_up(m, ps):
                t1 = t1r.next()
                op("act", lambda e: e.activation(out=t1.t[:, 0:n], in_=ps.t[:, 0:n], func=AF.Relu),
                   reads=[ps.R], writes=[t1.R])
                op("dve", lambda e: e.tensor_tensor(out=hT.t[:, m, 0:n], in0=t1.t[:, 0:n], in1=t1.t[:, 0:n],
                                                    op=ALU.mult), reads=[t1.R], writes=[hT.rs[m]])
            linear(n, uT, lambda k: U(k, n), w_up[li], 0, g * 2048, 16, ev_up)

            def ev_dn(m, ps, g=g):
                if g == 0:
                    op("act", lambda e: e.activation(out=yT.t[:, m, 0:n], in_=ps.t[:, 0:n], func=AF.Copy),
                       reads=[ps.R], writes=[yT.rs[m]])
                else:
                    op("dve", lambda e: e.tensor_tensor(out=yT.t[:, m, 0:n], in0=yT.t[:, m, 0:n],
                                                        in1=ps.t[:, 0:n], op=ALU.add), reads=[ps.R], writes=[yT.rs[m]])
            linear(n, hT, lambda k: hT.t[:, k, 0:n], w_dn[li], g * 2048, 0, 16, ev_dn)

    def memkv():
        n = NMEM
        dma("sp", [(xT.t[:, :, 0:n], memin[:, :, :])], writes=xT.rs, owner=stR)
        for li in range(NLW):
            with kb.phase():
                prenorm(n, li, memg)
                for which, wd, od in ((0, xa_wk, mk_out), (1, xa_wv, mv_out)):
                    for cb in range(8):
                        wb = wload([(wview(wd[li], 0, cb * 256, 256), 0)])
                        for tb in range(2):
                            ps = bank()
                            mm(ps, slice(0, 256), [U(k, 128, tb * 128) for k in range(KC)],
                               [wb.t[:, k, :] for k in range(KC)], [wb.R] + uT.rs)
                            c = tb * 8 + cb
                            op("act", lambda e, ps=ps, c=c: e.activation(out=yT.t[:, c, 0:256], in_=ps.t[:, 0:256], func=AF.Copy),
                               reads=[ps.R], writes=[yT.rs[c]])
                        if which == 0:
                            for mi in range(2):
                                m = cb * 2 + mi
                                ps = bank()
                                mm(ps, slice(0, n), [wb.t[:, k, mi * 128:(mi + 1) * 128] for k in range(KC)],
                                   [U(k, n) for k in range(KC)], [wb.R] + uT.rs)
                                op("act", lambda e, ps=ps, m=m: e.activation(out=hT.t[:, m, 0:n], in_=ps.t[:, 0:n], func=AF.Copy),
                                   reads=[ps.R], writes=[hT.rs[m]])
                    for tb in range(2):
                        dma("sp", [(od[li, tb * 128:(tb + 1) * 128, :].rearrange("p (c d) -> p c d", d=256),
                                    yT.t[:, tb * 8:(tb + 1) * 8, 0:256])], reads=yT.rs[tb * 8:(tb + 1) * 8],
                            writes=([mvR[li]] if which == 1 else []), owner=outR)
                    if which == 0:
                        dma("sp", [(kT_scr[li], hT.t[:, :, 0:n])], reads=hT.rs, writes=[kTR[li]], owner=outR)

    def run_layers(n, mode, tile):
        subs = []
        for li in range(NL):
            if cfg.get("mixer", True):
                subs.append((li, "mixer"))
            if cfg.get("xattn", True):
                subs.append((li, "xattn"))
            if cfg.get("mlp", True):
                subs.append((li, "mlp"))
        gidx = {"mixer": 0, "xattn": 2, "mlp": 4}
        for si, (li, sk) in enumerate(subs):
            kind = LT[li]
            g_pre = li * 6 + gidx[sk]
            g_next = None
            if si + 1 < len(subs):
                g_next = subs[si + 1][0] * 6 + gidx[subs[si + 1][1]]
            with kb.phase():
                if si == 0:
                    prenorm(n, g_pre)
                if sk == "mixer":
                    if kind == "hgrn":
                        hgrn(n, mode, tile, li, li // 3)
                        linear(n, hT, lambda k: hT.t[:, k, 0:n], hg_w_o[li // 3], 0, 0, 16, evac_copy(yT, n))
                    elif kind == "gla":
                        gla(n, mode, tile)
                        linear(n, hT, lambda k: hT.t[:, k, 0:n], gl_w_o[0], 0, 0, 16, evac_copy(yT, n))
                    else:
                        pool_state_out(n, mode, tile, g_pre)
                        pool(n, mode, tile)
                elif sk == "xattn":
                    xattn(n, mode, li)
                else:
                    mlp(n, li)
                postnorm_add(n, g_pre + 1, g_next)

    if cfg.get("xattn", True):
        memkv()
    if SAMPLE:
        dma("sp", [(xT.t[:, :, 0:NS], xsin[:, :, :])], writes=xT.rs, owner=stR)
        run_layers(NS, "s", -1)
        dma("sp", [(ys_out[:, :, :], xT.t[:, :, 0:NS])], reads=xT.rs, owner=outR)
    for tile in range(NT):
        t0 = tile * T
        dma("sp", [(xT.t[:, :, 0:T], xin[:, :, t0:t0 + T])], writes=xT.rs, owner=stR)
        run_layers(T, "p", tile)
        dma("sp", [(y_out[:, :, t0:t0 + T], xT.t[:, :, 0:T])], reads=xT.rs, owner=outR)
    kb.barrier(engines=("sp",))
    return nc, es


def _fm(a):
    t = a.shape[0]
    return np.ascontiguousarray(a.T.reshape(KC, 128, t).transpose(1, 0, 2))


def _consts():
    cst = np.zeros((128, 6, 128), np.float32)
    cst[:, 0, :] = np.eye(128)
    cst[:, 1, :] = 1.0 / 2048
    cst[:, 2, :] = 1.0 / 128
    cst[:, 3, :] = 1.0 / 512
    cst[:, 4, :] = 1.0
    cst[:, 5, :] = np.triu(np.ones((128, 128)))
    cm32 = np.zeros((32, 16, 32), np.float32)
    cm32[:, :, :] = np.triu(np.ones((32, 32)))[:, None, :]
    rst = np.ones((128, T), np.float32)
    rst[:, 0::32] = 0.0
    icnt = np.zeros((128, 4, 16), np.float32)
    for g, w in enumerate((2, 4, 8, 16)):
        icnt[:, g, :] = 1.0 / np.minimum(w, np.arange(16) + 1)[None, :]
    dmask = np.zeros((16, 16, 128), np.float32)
    for s in range(16):
        dmask[s, s, :] = 1.0
    psel = np.zeros((240, 4, NS), np.float32)
    for b in range(NS):
        for r in range(15):
            for g, w in enumerate((2, 4, 8, 16)):
                if r >= 15 - (w - 1):
                    psel[b * 15 + r, g, b] = 1.0 / w
    psel = np.ascontiguousarray(psel.reshape(2, 120, 4, NS).transpose(1, 0, 2, 3))
    return dict(cst=cst, cm32=cm32, rst=rst, icnt=icnt, dmask=dmask, psel=psel)


_CACHE = {}


def kernel(**inp):
    cfg = inp.pop("_cfg", {})
    f32 = lambda a: np.ascontiguousarray(np.asarray(a, dtype=np.float32))
    inp = {k: np.asarray(v) for k, v in inp.items()}
    key = repr(sorted(cfg.items()))
    if key not in _CACHE:
        _CACHE[key] = build(cfg)
    nc, es = _CACHE[key]
    consts = _consts()
    ng = inp["norm_gains"]
    shared = dict(consts)
    shared["gains"] = f32(ng.reshape(24, KC, 128).transpose(2, 0, 1))
    shared["memg"] = f32(inp["mem_norm"].reshape(4, KC, 128).transpose(2, 0, 1))
    shared["lbT"] = f32(inp["hgrn_lb"].reshape(4, 16, 128).transpose(2, 0, 1))
    shared["hgn"] = f32(inp["hgrn_g_norm"].T)
    shared["ggn"] = f32(inp["gla_g_norm"].reshape(4, 128).T)
    shared["gbk"] = f32(inp["gla_b_gk"].reshape(8, 128).T)
    shared["psc"] = f32(inp["pool_scale"].reshape(KC, 128).T)
    NLW = cfg.get("nlw", 4)
    NCORES = cfg.get("ncores", 8)
    for k_ in ("gla_w_in", "gla_w_gk1", "gla_w_gk2", "gla_w_o", "pool_w"):
        shared[k_] = f32(inp[k_])
    for k_ in ("xa_w_q", "xa_w_k", "xa_w_v", "xa_w_o", "mlp_w_up", "mlp_w_down"):
        shared[k_] = f32(inp[k_][:NLW])
    for k_ in ("hgrn_w_in", "hgrn_w_o"):
        shared[k_] = f32(inp[k_][:min(2, NLW)])
    in_maps = []
    for c in range(NCORES):
        b = c // 2
        s0 = c * NS
        m = dict(shared)
        m["xT"] = _fm(inp["x_prompt"][b])
        m["xsT"] = _fm(inp["x_sample"][s0:s0 + NS, 0, :])
        m["memT"] = _fm(inp["mem_prompt"][b])
        m["state_hgrn"] = f32(inp["state_hgrn"][:, s0:s0 + NS])
        m["state_gla"] = f32(inp["state_gla"][:, s0:s0 + NS])
        m["state_pool"] = f32(inp["state_pool"][:, s0:s0 + NS])
        m["cache_k"] = f32(inp["cache_mem_k"][:NLW, s0:s0 + NS].reshape(NLW, NS, NMEM, D))
        m["cache_v"] = f32(inp["cache_mem_v"][:NLW, s0:s0 + NS].reshape(NLW, NS, NMEM, D))
        in_maps.append(m)
    if cfg.get("trace", False):
        rr = run_bass_kernel_spmd(nc, in_maps, core_ids=list(range(NCORES)), trace=True)
        print("EXEC_TIME_NS", rr.exec_time_ns)
    else:
        rr = run_bass_kernel_spmd(nc, in_maps, core_ids=list(range(NCORES)))
    res = rr.results
    if NCORES < 8:
        res = list(res) + [res[c % NCORES] for c in range(NCORES, 8)]

    def unfm(a):
        return np.ascontiguousarray(a.transpose(2, 1, 0).reshape(a.shape[2], D))
    y_p = np.stack([unfm(res[2 * b]["yT"]) for b in range(4)])
    y_s = np.concatenate([unfm(res[c]["ysT"]) for c in range(8)])[:, None, :]
    h_p = np.stack([res[2 * b]["hp"] for b in range(4)], axis=1)
    h_s = np.concatenate([res[c]["hs"] for c in range(8)], axis=1)
    g_p = np.stack([res[2 * b]["gp"] for b in range(4)], axis=1)
    g_s = np.concatenate([res[c]["gs"] for c in range(8)], axis=1)
    p_p = np.stack([unfm(res[2 * b]["ppT"]) for b in range(4)])[None]
    p_s = np.concatenate([np.concatenate([res[c]["ps_shift"], unfm(res[c]["ps_newT"])[:, None, :]], axis=1)
                          for c in range(8)])[None]
    mk = np.stack([res[2 * b]["mk"] for b in range(4)], axis=1).reshape(4, 4, NMEM, 4, 512)
    mv = np.stack([res[2 * b]["mv"] for b in range(4)], axis=1).reshape(4, 4, NMEM, 4, 512)
    outs = (y_p, y_s, h_p, h_s, g_p, g_s, p_p, p_s, mk, mv)
    return tuple(np.ascontiguousarray(o, dtype=np.float32) for o in outs)
```
